# Optimizing a Trainium2 kernel written in Bass

```python
import jax, jax.numpy as jnp
from jax import lax
import numpy as np

D_MODEL = 4096
BATCH = 2
SEQ = 8192
DEPTH = 4

N_META = 16
GRID_W = 64
EPS = 1e-6
D_FF = 5632
ATTN_HEADS = 16
ATTN_KV_HEADS = 4
HEAD_DIM = 128
Q_BLOCK = 128
ROPE_THETA = 10000.0
FOURIER_GROUPS = 16
FOURIER_GROUP_DIM = 128
GLA_HEADS = 8
GLA_DK = D_MODEL // (2 * GLA_HEADS)
GLA_DV = D_MODEL // GLA_HEADS
GLA_GATE_RANK = 16
GLA_GATE_TAU = 16.0
GLA_CHUNK = 64

ATTN_Q_W = ATTN_HEADS * HEAD_DIM
ATTN_KV_W = ATTN_KV_HEADS * HEAD_DIM
FOURIER_W = FOURIER_GROUPS * FOURIER_GROUP_DIM
EVEN_IN_W = ATTN_Q_W + 2 * ATTN_KV_W + FOURIER_W
EVEN_MIX_W = ATTN_Q_W + FOURIER_W
GLA_K_W = GLA_HEADS * GLA_DK
GLA_V_W = GLA_HEADS * GLA_DV
ODD_IN_W = 2 * GLA_K_W + 2 * GLA_V_W

kernel_name = 'hybrid_gqa_fourier_gla_macaron_encoder'


def rmsnorm(x, g):
    xf = x.astype(jnp.float32)
    y = xf * lax.rsqrt(jnp.mean(xf * xf, axis=-1, keepdims=True) + EPS)
    return (y * g.astype(jnp.float32)).astype(x.dtype)


def swiglu(x, w_gate, w_up, w_down):
    return (jax.nn.silu(x @ w_gate) * (x @ w_up)) @ w_down


def axial_rope_tables(n_real):
    n_rows = n_real // GRID_W
    rows = jnp.repeat(jnp.arange(n_rows, dtype=jnp.float32), GRID_W)
    cols = jnp.tile(jnp.arange(GRID_W, dtype=jnp.float32), n_rows)
    n_freq = HEAD_DIM // 4
    inv_freq = jnp.power(ROPE_THETA, -jnp.arange(n_freq, dtype=jnp.float32) / n_freq)
    ang = jnp.stack([rows[:, None] * inv_freq, cols[:, None] * inv_freq], axis=1)
    ang = jnp.concatenate([jnp.zeros((N_META, 2, n_freq), jnp.float32), ang], axis=0)[:, None]
    return jnp.cos(ang), jnp.sin(ang)


def apply_axial_rope(x, cos, sin):
    shp = x.shape
    xr = x.astype(jnp.float32).reshape(shp[:-1] + (2, 2, HEAD_DIM // 4))
    x0, x1 = xr[..., 0, :], xr[..., 1, :]
    y = jnp.stack([x0 * cos - x1 * sin, x0 * sin + x1 * cos], axis=-2)
    return y.reshape(shp).astype(x.dtype)


def block_attention(q, k, v):
    B, L = q.shape[0], q.shape[1]
    n_pad = (-L) % Q_BLOCK
    pad = lambda a: jnp.pad(a, ((0, 0), (n_pad, 0), (0, 0), (0, 0)))
    q, k, v = pad(q), pad(k), pad(v)
    Lp = L + n_pad
    nb = Lp // Q_BLOCK
    G = ATTN_HEADS // ATTN_KV_HEADS
    scale = HEAD_DIM ** -0.5
    key_bias = jnp.where(jnp.arange(Lp) < n_pad, -1e30, 0.0).astype(jnp.float32)
    qb = q.reshape(B, nb, Q_BLOCK, ATTN_KV_HEADS, G, HEAD_DIM).transpose(1, 0, 2, 3, 4, 5)

    def one_block(q_blk):
        s = jnp.einsum('bqkgd,bskd->bkgqs', q_blk, k, preferred_element_type=jnp.float32) * scale + key_bias
        p = jax.nn.softmax(s, axis=-1).astype(v.dtype)
        return jnp.einsum('bkgqs,bskd->bqkgd', p, v)

    o = lax.map(one_block, qb)
    o = o.transpose(1, 0, 2, 3, 4, 5).reshape(B, Lp, ATTN_Q_W)
    return o[:, n_pad:]


def fourier_mix(f):
    B, L = f.shape[0], f.shape[1]
    fg = f.astype(jnp.float32).reshape(B, L, FOURIER_GROUPS, FOURIER_GROUP_DIM)
    y = jnp.fft.fft2(fg, axes=(1, 3), norm='ortho').real
    return y.reshape(B, L, FOURIER_W).astype(f.dtype)


def even_mixer(h, w_in, q_norm, k_norm, w_out, cos, sin):
    B, L, _ = h.shape
    u = h @ w_in
    q, k, v, f = jnp.split(u, [ATTN_Q_W, ATTN_Q_W + ATTN_KV_W, ATTN_Q_W + 2 * ATTN_KV_W], axis=-1)
    q = apply_axial_rope(rmsnorm(q.reshape(B, L, ATTN_HEADS, HEAD_DIM), q_norm), cos, sin)
    k = apply_axial_rope(rmsnorm(k.reshape(B, L, ATTN_KV_HEADS, HEAD_DIM), k_norm), cos, sin)
    v = v.reshape(B, L, ATTN_KV_HEADS, HEAD_DIM)
    o = jnp.concatenate([block_attention(q, k, v), fourier_mix(f)], axis=-1)
    return o @ w_out


def gla_chunk_scan(q, k, v, log_a):
    B, Lp, H, DK = q.shape
    DV = v.shape[-1]
    C = GLA_CHUNK
    nc = Lp // C
    to_chunks = lambda a: a.reshape(B, nc, C, H, a.shape[-1]).transpose(1, 0, 3, 2, 4)
    causal = jnp.tril(jnp.ones((C, C), dtype=bool))[:, :, None]

    def step(S, xs):
        qi, ki, vi, gi = xs
        qf, kf, vf = qi.astype(jnp.float32), ki.astype(jnp.float32), vi.astype(jnp.float32)
        b = jnp.cumsum(gi, axis=2)
        inter = jnp.einsum('bhcd,bhde->bhce', qf * jnp.exp(b), S)
        diff = b[:, :, :, None, :] - b[:, :, None, :, :]
        decay = jnp.exp(jnp.where(causal, diff, -jnp.inf))
        A = jnp.einsum('bhid,bhjd,bhijd->bhij', qf, kf, decay)
        intra = jnp.einsum('bhij,bhje->bhie', A, vf)
        b_last = b[:, :, -1:, :]
        S_new = jnp.exp(b_last[:, :, 0, :])[..., None] * S + jnp.einsum('bhcd,bhce->bhde', kf * jnp.exp(b_last - b), vf)
        return S_new, (inter + intra).astype(v.dtype)

    S0 = jnp.zeros((B, H, DK, DV), jnp.float32)
    _, o = lax.scan(step, S0, (to_chunks(q), to_chunks(k), to_chunks(v), to_chunks(log_a)))
    return o.transpose(1, 0, 3, 2, 4).reshape(B, Lp, H, DV)


def odd_mixer(h, w_in, gate_a, gate_b, gate_bias, head_norm, w_out):
    B, L, _ = h.shape
    u = h @ w_in
    q, k, v, r = jnp.split(u, [GLA_K_W, 2 * GLA_K_W, 2 * GLA_K_W + GLA_V_W], axis=-1)
    q = q.reshape(B, L, GLA_HEADS, GLA_DK) * (GLA_DK ** -0.5)
    k = k.reshape(B, L, GLA_HEADS, GLA_DK)
    v = v.reshape(B, L, GLA_HEADS, GLA_DV)
    low = jnp.einsum('bld,ndr->nblr', h, gate_a)
    z = jnp.einsum('nblr,nrk->nblk', low, gate_b) + gate_bias[:, None, None, :]
    log_a = (jax.nn.log_sigmoid(z.astype(jnp.float32)) / GLA_GATE_TAU).reshape(2, B, L, GLA_HEADS, GLA_DK)
    n_pad = (-L) % GLA_CHUNK
    pad = lambda a: jnp.pad(a, ((0, 0), (n_pad, 0), (0, 0), (0, 0)))
    q, k, v, g_fw, g_bw = pad(q), pad(k), pad(v), pad(log_a[0]), pad(log_a[1])
    flip = lambda a: a[:, ::-1]
    o_fw = gla_chunk_scan(q, k, v, g_fw)
    o_bw = flip(gla_chunk_scan(flip(q), flip(k), flip(v), flip(g_bw)))
    o = rmsnorm((o_fw + o_bw)[:, n_pad:], head_norm)
    o = o.reshape(B, L, GLA_V_W) * jax.nn.silu(r)
    return o @ w_out


def setup_inputs(seed: int = 0) -> dict:
    key = jax.random.key(seed)
    ks = jax.random.split(key, 16)
    n_even = (DEPTH + 1) // 2
    n_odd = DEPTH // 2
    nrm = lambda k, shape, fan_in: jax.random.normal(k, shape, jnp.float32) * (fan_in ** -0.5)
    gain = lambda k, shape: 1.0 + 0.02 * jax.random.normal(k, shape, jnp.float32)
    return {
        'x': jax.random.normal(ks[0], (BATCH, SEQ, D_MODEL), jnp.float32),
        'meta_tokens': jax.random.normal(ks[1], (N_META, D_MODEL), jnp.float32),
        'pre_norm': gain(ks[2], (DEPTH, 3, D_MODEL)),
        'ffn_w_gate': nrm(ks[3], (DEPTH, 2, D_MODEL, D_FF), D_MODEL),
        'ffn_w_up': nrm(ks[4], (DEPTH, 2, D_MODEL, D_FF), D_MODEL),
        'ffn_w_down': nrm(ks[5], (DEPTH, 2, D_FF, D_MODEL), D_FF),
        'even_w_in': nrm(ks[6], (n_even, D_MODEL, EVEN_IN_W), D_MODEL),
        'even_q_norm': gain(ks[7], (n_even, HEAD_DIM)),
        'even_k_norm': gain(ks[8], (n_even, HEAD_DIM)),
        'even_w_out': nrm(ks[9], (n_even, EVEN_MIX_W, D_MODEL), EVEN_MIX_W),
        'odd_w_in': nrm(ks[10], (n_odd, D_MODEL, ODD_IN_W), D_MODEL),
        'odd_gate_a': nrm(ks[11], (n_odd, 2, D_MODEL, GLA_GATE_RANK), D_MODEL),
        'odd_gate_b': nrm(ks[12], (n_odd, 2, GLA_GATE_RANK, GLA_K_W), GLA_GATE_RANK),
        'odd_gate_bias': 0.1 * jax.random.normal(ks[13], (n_odd, 2, GLA_K_W), jnp.float32),
        'odd_head_norm': gain(ks[14], (n_odd, GLA_DV)),
        'odd_w_out': nrm(ks[15], (n_odd, GLA_V_W, D_MODEL), GLA_V_W),
    }


def reference(x, meta_tokens, pre_norm, ffn_w_gate, ffn_w_up, ffn_w_down, even_w_in, even_q_norm, even_k_norm, even_w_out, odd_w_in, odd_gate_a, odd_gate_b, odd_gate_bias, odd_head_norm, odd_w_out):
    B, n_real, _ = x.shape
    cos, sin = axial_rope_tables(n_real)
    meta = jnp.broadcast_to(meta_tokens.astype(x.dtype)[None], (B, N_META, D_MODEL))
    h = jnp.concatenate([meta, x], axis=1)
    for layer in range(DEPTH):
        i = layer // 2
        h = h + 0.5 * swiglu(rmsnorm(h, pre_norm[layer, 0]), ffn_w_gate[layer, 0], ffn_w_up[layer, 0], ffn_w_down[layer, 0])
        hn = rmsnorm(h, pre_norm[layer, 1])
        if layer % 2 == 0:
            h = h + even_mixer(hn, even_w_in[i], even_q_norm[i], even_k_norm[i], even_w_out[i], cos, sin)
        else:
            h = h + odd_mixer(hn, odd_w_in[i], odd_gate_a[i], odd_gate_b[i], odd_gate_bias[i], odd_head_norm[i], odd_w_out[i])
        h = h + 0.5 * swiglu(rmsnorm(h, pre_norm[layer, 2]), ffn_w_gate[layer, 1], ffn_w_up[layer, 1], ffn_w_down[layer, 1])
    return h[:, N_META:]
```

```python
import numpy as np
import concourse.bass as bass
import concourse.mybir as mybir
from concourse.bass_utils import run_bass_kernel_spmd

F32 = mybir.dt.float32
BF16 = mybir.dt.bfloat16
AF = mybir.ActivationFunctionType
ALU = mybir.AluOpType
NCORES = 8
WC = 256


class Cfg:
    def __init__(s, D=4096, DFF=5632, NREAL=8192, NMETA=16, DEPTH=4, NH=16, NKV=4, FG=16,
                 GH=8, GDK=256, GDV=512, GR=16, L1=108, L2=76, TT=342, GRID_W=64, stages="all"):
        s.D, s.DFF, s.NREAL, s.NMETA, s.DEPTH = D, DFF, NREAL, NMETA, DEPTH
        s.NH, s.NKV, s.FG, s.GH, s.GDK, s.GDV, s.GR = NH, NKV, FG, GH, GDK, GDV, GR
        s.L1, s.L2, s.TT, s.GRID_W = L1, L2, TT, GRID_W
        s.L = NREAL + NMETA
        s.TPC = s.L // 4
        s.NT = s.TPC // TT
        assert s.NT * TT == s.TPC and L1 * L2 == s.L and (L2 // 4) * L1 == s.TPC
        s.KC = D // 128
        s.FC = DFF // 128
        s.QW, s.KVW, s.FW = NH * 128, NKV * 128, FG * 128
        s.EIN = s.QW + 2 * s.KVW + s.FW
        s.EMIX = s.QW + s.FW
        s.GKW, s.GVW = GH * GDK, GH * GDV
        s.OIN = 2 * s.GKW + 2 * s.GVW
        s.stages = stages


class Buf:
    __slots__ = ("w", "rc", "rd", "name")

    def __init__(self, name):
        self.w = None
        self.rc = {}
        self.rd = []
        self.name = name


class Prog:
    COMPUTE = ("pe", "act", "dve")
    DMAQ = ("sp", "pool")
    R = 12

    def __init__(self, nc):
        self.nc = nc
        self.ops = {e: [] for e in self.COMPUTE + self.DMAQ}
        self.bufs = {}
        self.pending_fence = {}
        self.fence_deps = set()
        self.jcache = None

    def B(self, name):
        b = self.bufs.get(name)
        if b is None:
            b = self.bufs[name] = Buf(name)
        return b

    def op(self, eng, fn, reads=(), writes=(), inc=16):
        ops = self.ops[eng]
        idx = len(ops)
        me = (eng, idx)
        deps = set()
        for b in reads:
            if b.w is not None:
                deps.add(b.w)
        for b in writes:
            if b.w is not None:
                deps.add(b.w)
            for e2, i2 in b.rc.items():
                deps.add((e2, i2))
            deps.update(b.rd)
        if self.pending_fence.get(eng):
            deps.update(self.fence_deps)
            self.pending_fence[eng] = False
        deps.discard(me)
        isdma = eng in self.DMAQ
        for b in writes:
            b.w = me
            b.rc = {}
            b.rd = []
        for b in reads:
            if b.w == me:
                continue
            if isdma:
                b.rd.append(me)
            else:
                b.rc[eng] = idx
        ops.append({"fn": fn, "deps": deps, "sig": False, "inc": inc})
        return me

    def fence(self):
        self.pending_fence = {e: True for e in self.ops}
        self.fence_deps = set()
        for e in self.COMPUTE:
            if self.ops[e]:
                self.fence_deps.add((e, len(self.ops[e]) - 1))
        for e in self.DMAQ:
            n = len(self.ops[e])
            for i in range(max(0, n - self.R), n):
                self.fence_deps.add((e, i))

    def emit(self):
        nc = self.nc
        ops = self.ops
        for eng, lst in ops.items():
            for i, o in enumerate(lst):
                best = {}
                keep = []
                for (e2, i2) in o["deps"]:
                    if e2 in self.COMPUTE:
                        if e2 == eng and eng == "pe":
                            continue
                        if i2 > best.get(e2, -1):
                            best[e2] = i2
                    else:
                        keep.append((e2, i2))
                o["deps"] = [(e2, i2) for e2, i2 in best.items()] + keep
                for (e2, i2) in o["deps"]:
                    ops[e2][i2]["sig"] = True
        val = {}
        for eng in self.COMPUTE:
            c = 0
            for i, o in enumerate(ops[eng]):
                if o["sig"]:
                    c += 1
                val[(eng, i)] = c
        slot_of = {}
        pre_of = {}
        for eng in self.DMAQ:
            cum = [0] * self.R
            for i, o in enumerate(ops[eng]):
                s = i % self.R
                pre_of[(eng, i)] = cum[s]
                cum[s] += o["inc"]
                val[(eng, i)] = cum[s]
                slot_of[(eng, i)] = s
        import contextlib
        with contextlib.ExitStack() as st:
            sems = {}
            for eng in self.COMPUTE:
                sems[eng] = st.enter_context(nc.semaphore("s_" + eng))
            for eng in self.DMAQ:
                for s in range(self.R):
                    sems[(eng, s)] = st.enter_context(nc.semaphore("s_%s%d" % (eng, s)))
            block = st.enter_context(nc.Block())

            def run(eng, e):
                waited = {}
                self.jcache = None
                for i, o in enumerate(ops[eng]):
                    wl = []
                    for (e2, i2) in o["deps"]:
                        key = e2 if e2 in self.COMPUTE else (e2, slot_of[(e2, i2)])
                        wl.append((key, val[(e2, i2)]))
                    if eng in self.DMAQ and pre_of[(eng, i)] > 0:
                        wl.append(((eng, slot_of[(eng, i)]), pre_of[(eng, i)]))
                    for key, v in wl:
                        if waited.get(key, 0) < v:
                            e.wait_ge(sems[key], v)
                            waited[key] = v
                    ins = o["fn"](e)
                    if eng in self.DMAQ:
                        if o["inc"] == 16:
                            ins.then_inc(sems[(eng, slot_of[(eng, i)])], 16)
                        else:
                            ins.then_inc(sems[(eng, slot_of[(eng, i)])])
                    elif o["sig"]:
                        ins.then_inc(sems[eng], 1)
                if eng in self.DMAQ:
                    n = len(ops[eng])
                    for i in range(max(0, n - self.R), n):
                        key = (eng, slot_of[(eng, i)])
                        if waited.get(key, 0) < val[(eng, i)]:
                            e.wait_ge(sems[key], val[(eng, i)])
                            waited[key] = val[(eng, i)]

            block.tensor(lambda e: run("pe", e))
            block.scalar(lambda e: run("act", e))
            block.vector(lambda e: run("dve", e))
            block.sync(lambda e: run("sp", e))
            block.gpsimd(lambda e: run("pool", e))


def weight_specs(cfg):
    sp = []
    for l in range(cfg.DEPTH):
        for i in range(2):
            sp.append(("wg%d_%d" % (l, i), cfg.D, cfg.DFF))
            sp.append(("wu%d_%d" % (l, i), cfg.D, cfg.DFF))
            sp.append(("wd%d_%d" % (l, i), cfg.DFF, cfg.D))
        if l % 2 == 0:
            sp.append(("win%d" % l, cfg.D, cfg.EIN))
            sp.append(("wout%d" % l, cfg.EMIX, cfg.D))
        else:
            sp.append(("win%d" % l, cfg.D, cfg.OIN))
            sp.append(("wout%d" % l, cfg.GVW, cfg.D))
    return sp


def tile_weight(W):
    K, M = W.shape
    kc, nb = K // 128, M // WC
    return np.ascontiguousarray(W.reshape(kc, 128, nb, WC).transpose(2, 1, 0, 3)).reshape(nb * 128, kc * WC)


CHUNK_ORDER = [0, 4, 1, 5, 2, 6, 3, 7]
PAIRS = [[0, 4], [1, 5], [2, 6], [3, 7]]
QUADS = [[0, 1, 2, 3], [4, 5, 6, 7]]
CW = 2048


class Ctx:
    pass


def flat(t):
    return t.ap().rearrange("a b -> (a b)")


def build(cfg):
    nc = bass.Bass("TRN2", target_bir_lowering=False)
    P = Prog(nc)
    c = Ctx()
    c.cfg, c.nc, c.P = cfg, nc, P
    D, KC, FC, TT, TPC, L = cfg.D, cfg.KC, cfg.FC, cfg.TT, cfg.TPC, cfg.L
    ne, no = (cfg.DEPTH + 1) // 2, cfg.DEPTH // 2
    c.HPC = cfg.GH // 4
    c.NDK, c.NDV = cfg.GDK // 128, cfg.GDV // 128
    c.GCB = 16 if cfg.NREAL >= 4096 else 4
    def ext(name, shape):
        return nc.dram_tensor(name, shape, F32, kind="ExternalInput")
    c.xT = ext("xT", [D, TPC]).ap()
    c.outT = nc.dram_tensor("outT", [D, TPC], F32, kind="ExternalOutput").ap()
    c.pn = ext("pn", [128, cfg.DEPTH * 3 * KC]).ap()
    c.rope = ext("rope", [128, 2 * TPC]).ap()
    c.rotm_d = ext("rotm", [128, 128]).ap()
    c.qkg_d = ext("qkg", [128, 2 * max(ne, 1)]).ap()
    c.dft_a = ext("dft_a", [cfg.L1, 2 * cfg.L1]).ap()
    c.dft_w = ext("dft_w", [cfg.L1, 2 * cfg.L2]).ap()
    c.dft_b = ext("dft_b", [cfg.L2, 3 * (cfg.L2 // 4)]).ap()
    c.dft_c = ext("dft_c", [128, 256]).ap()
    c.gla_m = ext("gla_m", [64, 6 * 64]).ap()
    c.ga_d = ext("ga", [128, max(no, 1) * 2 * KC * 16]).ap()
    c.gb_d = ext("gb", [17, max(no, 1) * 2 * cfg.GKW]).ap()
    c.hg_d = ext("hg", [128, max(no, 1) * c.NDV]).ap()
    c.hT = nc.dram_tensor("hT", [D, TPC], F32).ap()
    c.wsh, c.wsend, c.wmid, c.wfull, c.wcols = {}, {}, {}, {}, {}
    for name, K, M in weight_specs(cfg):
        rows, cols = (M // WC) * 128, (K // 128) * WC
        n8 = rows * cols // 8
        c.wcols[name] = cols
        c.wsh[name] = ext(name, [n8 // CW, CW])
        c.wsend[name] = nc.dram_tensor(name + "_s", [n8 // CW, CW], BF16)
        c.wmid[name] = nc.dram_tensor(name + "_m", [2 * n8 // CW, CW], BF16)
        c.wfull[name] = nc.dram_tensor(name + "_f", [8 * n8 // CW, CW], BF16)
    bf = lambda name, shape: nc.dram_tensor(name, shape, BF16)
    c.ev, c.od = {}, {}
    for l in range(cfg.DEPTH):
        if l % 2 == 0:
            c.ev[l] = dict(
                qT=bf("qT%d" % l, [cfg.QW, TPC]), ks=bf("ks%d" % l, [cfg.KVW, TPC]), kf=bf("kf%d" % l, [4 * cfg.KVW, TPC]),
                vs=bf("vs%d" % l, [cfg.NKV * TPC, 128]), vf=bf("vf%d" % l, [cfg.NKV * 4 * TPC, 128]),
                fs=bf("fs%d" % l, [cfg.FG * TPC, 128]), ff=bf("ff%d" % l, [cfg.FG * 4 * TPC, 128]),
                ao=bf("ao%d" % l, [cfg.EMIX, TPC]),
                twr=[bf("twr%d_%d" % (l, i), [cfg.L2, cfg.L1 * 128]) for i in range(2)],
                twi=[bf("twi%d_%d" % (l, i), [cfg.L2, cfg.L1 * 128]) for i in range(2)])
        else:
            d = {}
            for nm, shp in (("q", [cfg.GKW, TPC]), ("k", [cfg.GKW, TPC]), ("kt", [4 * 4 * TPC, cfg.GKW // 16]), ("vt", [4 * 8 * TPC, cfg.GVW // 32]),
                            ("gf", [4 * 4 * TPC, cfg.GKW // 16]), ("gb", [4 * 4 * TPC, cfg.GKW // 16])):
                d[nm + "s"] = bf("%ss%d" % (nm, l), shp)
                d[nm + "f"] = bf("%sf%d" % (nm, l), [4 * shp[0], shp[1]])
            d["ql"], d["kl"] = bf("ql%d" % l, [c.HPC * cfg.GDK * 4, TPC]), bf("kl%d" % l, [c.HPC * cfg.GDK * 4, TPC])
            d["ktl"], d["vtl"] = bf("ktl%d" % l, [4 * L, c.HPC * cfg.GDK // 4]), bf("vtl%d" % l, [8 * L, c.HPC * cfg.GDV // 8])
            d["gfl"], d["gbl"] = bf("gfl%d" % l, [4 * L, c.HPC * cfg.GDK // 4]), bf("gbl%d" % l, [4 * L, c.HPC * cfg.GDK // 4])
            d["ol"] = bf("ol%d" % l, [cfg.GVW, TPC])
            d["sr"] = bf("sr%d" % l, [cfg.GVW, TPC])
            d["ofw"] = bf("ofw%d" % l, [c.HPC * cfg.GDV, L])
            d["os"] = bf("os%d" % l, [c.HPC * cfg.GDV * 4, TPC])
            d["of"] = bf("of%d" % l, [cfg.GVW * 4, TPC])
            c.od[l] = d
    import contextlib
    with contextlib.ExitStack() as st:
        NB_, NF_ = 65000, 15200
        c.AB = st.enter_context(nc.sbuf_tensor("arena_b", [128, NB_], BF16))
        c.AF = st.enter_context(nc.sbuf_tensor("arena_f", [128, NF_], F32))
        c.NB_, c.NF_ = NB_, NF_
        c.pb = c.pf = 0
        c.ps = [st.enter_context(nc.psum_tensor("ps%d" % i, [128, 512], F32)) for i in range(8)]
        c.psi = 0
        program(c)
        P.emit()
    return nc


def ab(c, n):
    a = c.pb
    c.pb += n + (n & 1)
    assert c.pb <= c.NB_, ("bf16 arena", c.pb)
    return c.AB[:, a:a + n]


def af(c, n):
    a = c.pf
    c.pf += n
    assert c.pf <= c.NF_, ("f32 arena", c.pf)
    return c.AF[:, a:a + n]


def next_ps(c):
    i = c.psi
    c.psi = (c.psi + 1) % 8
    return c.ps[i], c.P.B("ps%d" % i)


def jv(c, key=0):
    if c.P.jcache is None:
        c.P.jcache = {}
    if key not in c.P.jcache:
        c.P.jcache[key] = c.nc.partition_id() % 4
    return c.P.jcache[key]


def wblk_view(c, name, blk):
    cols = c.wcols[name]
    return flat(c.wfull[name])[blk * 128 * cols:(blk + 1) * 128 * cols].rearrange("(p f) -> p f", p=128)


def load_wblk(c, name, blk):
    P = c.P
    i = c.wi
    c.wi ^= 1
    wb = c.wbuf[i]
    ncols = c.wcols[name]
    src = wblk_view(c, name, blk)
    P.op("sp", lambda e: e.dma_start(out=wb[:, 0:ncols], in_=src),
         reads=[P.B("wf_" + name)], writes=[P.B("wbuf%d" % i)])
    return wb, P.B("wbuf%d" % i)


WPE = 262144


def coll(c, groups, src_ap, dst_ap, rB, wB):
    P = c.P
    P.op("pool", (lambda e: e.collective_compute(
        "AllGather", ALU.bypass, replica_groups=groups, ins=[src_ap], outs=[dst_ap])),
        reads=[P.B(rB)], writes=[P.B(wB)], inc=1)


def phase0_weights(c):
    P = c.P
    for name, K, M in weight_specs(c.cfg):
        sh, snd, mid, full = c.wsh[name], c.wsend[name], c.wmid[name], c.wfull[name]
        rows = sh.shape[0]
        n8 = rows * CW
        pe = min(WPE, n8)
        nb = n8 // pe
        assert nb * pe == n8
        step = max(1, rows // 4)
        r0 = 0
        while r0 < rows:
            r1 = min(rows, r0 + step)
            P.op("pool", (lambda e, a=snd.ap()[r0:r1, :], b=sh.ap()[r0:r1, :]: e.dma_start(out=a, in_=b)),
                 writes=[P.B("ws_" + name)])
            r0 = r1
        fs_, fm_, ff_ = flat(snd), flat(mid), flat(full)
        for i in range(nb):
            coll(c, PAIRS, fs_[i * pe:(i + 1) * pe], fm_[i * 2 * pe:(i + 1) * 2 * pe], "ws_" + name, "wm_" + name)
        for i in range(nb):
            coll(c, QUADS, fm_[i * 2 * pe:(i + 1) * 2 * pe], ff_[i * 8 * pe:(i + 1) * 8 * pe], "wm_" + name, "wf_" + name)


def cag(c, snd, full, npieces, rB, wB):
    if "nocoll" in c.cfg.stages:
        return
    fs_, ff_ = flat(snd), flat(full)
    n = 1
    for d in snd.shape:
        n *= d
    pe = n // npieces
    assert pe * npieces == n and pe * 2 <= 1048576, (snd.name, pe)
    for i in range(npieces):
        coll(c, QUADS, fs_[i * pe:(i + 1) * pe], ff_[i * 4 * pe:(i + 1) * 4 * pe], rB, wB)


def tl_setup(c):
    cfg = c.cfg
    KC, FC, TT = cfg.KC, cfg.FC, cfg.TT
    c.pb, c.pf = c.pb0, c.pf0
    c.h = af(c, KC * TT).rearrange("p (k t) -> p k t", t=TT)
    c.rstd = af(c, TT)
    c.sg = [af(c, TT) for _ in range(2)]
    c.sgi = 0
    c.xg, c.t1, c.t2, c.rstdh = af(c, TT), af(c, TT), af(c, TT), af(c, TT)
    c.ropet = af(c, 2 * TT)
    c.ef = af(c, 512)
    MX = max(FC, KC, cfg.GVW // 128, cfg.EMIX // 128)
    c.hn = ab(c, max(KC, cfg.GVW // 128) * TT).rearrange("p (k t) -> p k t", t=TT)
    c.actT = ab(c, MX * TT).rearrange("p (k t) -> p k t", t=TT)
    c.wbuf = [ab(c, max(MX * WC, (cfg.GVW // 128) * TT)) for _ in range(2)]
    c.wi = 0
    c.sqh, c.xgb, c.yb = ab(c, TT), ab(c, TT), [ab(c, TT) for _ in range(2)]
    c.ybi = 0
    c.vst = [ab(c, 512) for _ in range(2)]
    c.vsti = 0
    c.lowx = ab(c, TT)
    c.srt = c.wbuf[1][:, 0:(cfg.GVW // 128) * TT].rearrange("p (k t) -> p k t", t=TT)


def rmsnorm_tile(c, gcol0):
    cfg, P = c.cfg, c.P
    KC, TT = cfg.KC, cfg.TT
    src, dst = c.h, c.hn
    hB, hnB, sqB, rsB = P.B("h"), P.B("hn"), P.B("actT"), P.B("rstd")
    sq = c.actT
    P.op("act", lambda e: e.activation(out=sq[:, 0:KC, :], in_=src[:, 0:KC, :], func=AF.Square),
         reads=[hB], writes=[sqB])
    ps, psB = next_ps(c)
    for kc in range(KC):
        P.op("pe", (lambda e, kc=kc: e.matmul(ps[:, 0:TT], lhsT=c.ones[:, :], rhs=sq[:, kc, :],
                                               start=(kc == 0), stop=(kc == KC - 1))),
             reads=[sqB, P.B("ones")], writes=[psB])
    rstd_from(c, ps, psB, c.rstd, rsB, 1.0 / cfg.D, TT)
    for kc in range(KC):
        P.op("dve", (lambda e, kc=kc: e.scalar_tensor_tensor(
            out=dst[:, kc, :], in0=src[:, kc, :], scalar=c.pn_sb[:, gcol0 + kc:gcol0 + kc + 1], in1=c.rstd[:, :],
            op0=ALU.mult, op1=ALU.mult)), reads=[hB, rsB, P.B("pn_sb")], writes=[hnB])


def rstd_from(c, ps, psB, dst, dB, inv_n, n):
    P = c.P
    P.op("dve", lambda e: e.tensor_scalar(out=dst[:, 0:n], in0=ps[:, 0:n], scalar1=inv_n, scalar2=1e-6,
                                          op0=ALU.mult, op1=ALU.add), reads=[psB], writes=[dB])
    P.op("act", lambda e: e.activation(out=dst[:, 0:n], in_=dst[:, 0:n], func=AF.Sqrt), reads=[dB], writes=[dB])
    P.op("dve", lambda e: e.reciprocal(out=dst[:, 0:n], in_=dst[:, 0:n]), reads=[dB], writes=[dB])


def fm_group(c, w, wB, sub, rhs3, rB, nk):
    P, TT = c.P, c.cfg.TT
    ps, psB = next_ps(c)
    for kc in range(nk):
        P.op("pe", (lambda e, kc=kc: e.matmul(
            ps[:, 0:TT], lhsT=w[:, kc * WC + sub * 128: kc * WC + sub * 128 + 128], rhs=rhs3[:, kc, :],
            start=(kc == 0), stop=(kc == nk - 1))), reads=[wB, rB], writes=[psB])
    return ps, psB


def subtiles(TT):
    r, t0 = [], 0
    while t0 < TT:
        n = min(128, TT - t0)
        r.append((t0, n))
        t0 += n
    return r


def tm_group(c, w, wB, tok0, ntok):
    P, KC = c.P, c.cfg.KC
    ps, psB = next_ps(c)
    for kc in range(KC):
        P.op("pe", (lambda e, kc=kc: e.matmul(
            ps[0:ntok, 0:WC], lhsT=c.hn[:, kc, tok0:tok0 + ntok], rhs=w[:, kc * WC:(kc + 1) * WC],
            start=(kc == 0), stop=(kc == KC - 1))), reads=[wB, c.P.B("hn")], writes=[psB])
    return ps, psB


def ffn_tile(c, l, i):
    cfg, P = c.cfg, c.P
    if cfg.stages == "noffn":
        return
    KC, FC, TT = cfg.KC, cfg.FC, cfg.TT
    hB, hnB, actB = P.B("h"), P.B("hn"), P.B("actT")
    rmsnorm_tile(c, (l * 3 + (0 if i == 0 else 2)) * KC)
    ng, nu, nd = "wg%d_%d" % (l, i), "wu%d_%d" % (l, i), "wd%d_%d" % (l, i)
    for blk in range(cfg.DFF // WC):
        wg, wgB = load_wblk(c, ng, blk)
        wu, wuB = load_wblk(c, nu, blk)
        for sub in range(WC // 128):
            fc = blk * (WC // 128) + sub
            pg, pgB = fm_group(c, wg, wgB, sub, c.hn, hnB, KC)
            pu, puB = fm_group(c, wu, wuB, sub, c.hn, hnB, KC)
            sg = c.sg[c.sgi]
            sgB = P.B("sg%d" % c.sgi)
            c.sgi ^= 1
            P.op("act", (lambda e, sg=sg, pg=pg: e.activation(out=sg[:, :], in_=pg[:, 0:TT], func=AF.Silu)),
                 reads=[pgB], writes=[sgB])
            P.op("dve", (lambda e, sg=sg, pu=pu, fc=fc: e.tensor_tensor(out=c.actT[:, fc, :], in0=sg[:, :],
                                                                       in1=pu[:, 0:TT], op=ALU.mult)),
                 reads=[sgB, puB], writes=[actB])
    for blk in range(cfg.D // WC):
        wd, wdB = load_wblk(c, nd, blk)
        for sub in range(WC // 128):
            dc = blk * (WC // 128) + sub
            ps, psB = fm_group(c, wd, wdB, sub, c.actT, actB, FC)
            P.op("dve", (lambda e, ps=ps, dc=dc: e.scalar_tensor_tensor(
                out=c.h[:, dc, :], in0=ps[:, 0:TT], scalar=0.5, in1=c.h[:, dc, :], op0=ALU.mult, op1=ALU.add)),
                reads=[psB, hB], writes=[hB])


def load_h(c, src, t):
    cfg, P = c.cfg, c.P
    TT = cfg.TT
    view = src.rearrange("(kc p) t -> p kc t", p=128)[:, :, t * TT:(t + 1) * TT]
    P.op("sp", lambda e: e.dma_start(out=c.h[:, :, :], in_=view),
         reads=[P.B("hsrc%s_%d" % (src.tensor.name, t))], writes=[P.B("h")])


def store_h(c, dst, t):
    cfg, P = c.cfg, c.P
    TT = cfg.TT
    view = dst.rearrange("(kc p) t -> p kc t", p=128)[:, :, t * TT:(t + 1) * TT]
    P.op("pool", lambda e: e.dma_start(out=view, in_=c.h[:, :, :]),
         reads=[P.B("h")], writes=[P.B("hsrc%s_%d" % (dst.tensor.name, t))])


def stage_out(c, ps, psB, npart, ncol, dst_ap, dstB, func=AF.Copy, split=1):
    P = c.P
    i = c.vsti
    c.vsti ^= 1
    st, stB = c.vst[i], P.B("vst%d" % i)
    P.op("act", lambda e: e.activation(out=st[0:npart, 0:ncol], in_=ps[0:npart, 0:ncol], func=func),
         reads=[psB], writes=[stB])
    src = st[0:npart, 0:ncol]
    if split > 1:
        src = src.rearrange("p (g d) -> p g d", g=split)
    P.op("pool", lambda e: e.dma_start(out=dst_ap, in_=src), reads=[stB], writes=[dstB])


def even_in_tile(c, l, t):
    cfg, P = c.cfg, c.P
    KC, TT = cfg.KC, cfg.TT
    e_i = l // 2
    ev = c.ev[l]
    hnB = P.B("hn")
    rmsnorm_tile(c, (l * 3 + 1) * KC)
    store_h(c, c.hT, t)
    if "norope" not in cfg.stages:
      P.op("sp", lambda e: e.dma_start(out=c.ropet.rearrange("p (a t) -> p a t", a=2),
                                      in_=c.rope.rearrange("p (a t) -> p a t", a=2)[:, :, t * TT:(t + 1) * TT]),
         writes=[P.B("ropet")])
    cosv, sinv = c.ropet[:, 0:TT], c.ropet[:, TT:2 * TT]
    name = "win%d" % l
    nqk = (cfg.QW + cfg.KVW) // WC
    for blk in range(nqk):
        w, wB = load_wblk(c, name, blk)
        for sub in range(2):
            hd = blk * 2 + sub
            isq = hd < cfg.NH
            gcol = 2 * e_i + (0 if isq else 1)
            ps, psB = fm_group(c, w, wB, sub, c.hn, hnB, KC)
            if "nochain" in cfg.stages:
                P.op("act", (lambda e, ps=ps: e.activation(out=c.sqh[:, :], in_=ps[:, 0:TT], func=AF.Square)),
                     reads=[psB], writes=[P.B("sqh")])
                continue
            P.op("dve", (lambda e, ps=ps, gcol=gcol: e.tensor_scalar(
                out=c.xg[:, :], in0=ps[:, 0:TT], scalar1=c.qkg[:, gcol:gcol + 1], scalar2=None, op0=ALU.mult)),
                reads=[psB, P.B("qkg")], writes=[P.B("xg")])
            P.op("act", (lambda e, ps=ps: e.activation(out=c.sqh[:, :], in_=ps[:, 0:TT], func=AF.Square)),
                 reads=[psB, P.B("xg")], writes=[P.B("sqh")])
            ps2, ps2B = next_ps(c)
            P.op("pe", (lambda e, ps2=ps2: e.matmul(ps2[:, 0:TT], lhsT=c.ones[:, :], rhs=c.sqh[:, :], start=True, stop=True)),
                 reads=[P.B("sqh"), P.B("ones")], writes=[ps2B])
            rstd_from(c, ps2, ps2B, c.rstdh, P.B("rstdh"), 1.0 / 128, TT)
            if "chain1" in cfg.stages:
                continue
            P.op("act", lambda e: e.activation(out=c.xgb[:, :], in_=c.xg[:, :], func=AF.Copy),
                 reads=[P.B("xg")], writes=[P.B("xgb")])
            ps3, ps3B = next_ps(c)
            P.op("pe", (lambda e, ps3=ps3: e.matmul(ps3[:, 0:TT], lhsT=c.rotm[:, :], rhs=c.xgb[:, :], start=True, stop=True)),
                 reads=[P.B("xgb"), P.B("rotm")], writes=[ps3B])
            if "chain2" in cfg.stages:
                P.op("dve", (lambda e, ps3=ps3: e.tensor_copy(out=c.t2[:, :], in_=ps3[:, 0:TT])), reads=[ps3B], writes=[P.B("t2")])
                continue
            P.op("dve", lambda e: e.tensor_tensor(out=c.t1[:, :], in0=c.xg[:, :], in1=cosv, op=ALU.mult),
                 reads=[P.B("xg"), P.B("ropet")], writes=[P.B("t1")])
            P.op("dve", (lambda e, ps3=ps3: e.tensor_tensor(out=c.t2[:, :], in0=ps3[:, 0:TT], in1=sinv, op=ALU.mult)),
                 reads=[ps3B, P.B("ropet")], writes=[P.B("t2")])
            P.op("dve", lambda e: e.tensor_tensor(out=c.t1[:, :], in0=c.t1[:, :], in1=c.t2[:, :], op=ALU.add),
                 reads=[P.B("t1"), P.B("t2")], writes=[P.B("t1")])
            yi = c.ybi
            c.ybi ^= 1
            yb, ybB = c.yb[yi], P.B("yb%d" % yi)
            P.op("dve", (lambda e, yb=yb: e.tensor_tensor(out=yb[:, :], in0=c.t1[:, :], in1=c.rstdh[:, :], op=ALU.mult)),
                 reads=[P.B("t1"), P.B("rstdh")], writes=[ybB])
            if isq:
                dst = ev["qT"].ap()[hd * 128:(hd + 1) * 128, t * TT:(t + 1) * TT]
                dB = P.B("qT%d" % l)
            else:
                hk = hd - cfg.NH
                dst = ev["ks"].ap()[hk * 128:(hk + 1) * 128, t * TT:(t + 1) * TT]
                dB = P.B("ks%d" % l)
            if "noqk" not in cfg.stages:
                P.op("pool", (lambda e, yb=yb, dst=dst: e.dma_start(out=dst, in_=yb[:, :])), reads=[ybB], writes=[dB])
    for blk in range(nqk, cfg.EIN // WC):
        if "novf" in cfg.stages:
            break
        w, wB = load_wblk(c, name, blk)
        col0 = blk * WC - (cfg.QW + cfg.KVW)
        for (tok0, ntok) in subtiles(TT):
            ps, psB = tm_group(c, w, wB, tok0, ntok)
            r0 = t * TT + tok0
            if col0 < cfg.KVW:
                g0 = col0 // 128
                dst, dB = ev["vs"].ap().rearrange("(g t) d -> t g d", t=cfg.TPC)[r0:r0 + ntok, g0:g0 + 2, :], P.B("vs%d" % l)
            else:
                g0 = (col0 - cfg.KVW) // 128
                dst, dB = ev["fs"].ap().rearrange("(g t) d -> t g d", t=cfg.TPC)[r0:r0 + ntok, g0:g0 + 2, :], P.B("fs%d" % l)
            stage_out(c, ps, psB, ntok, WC, dst, dB, split=2)


def even_out_tile(c, l, t):
    cfg, P = c.cfg, c.P
    TT = cfg.TT
    NK = cfg.EMIX // 128
    src = c.ev[l]["ao"].ap().rearrange("(kc p) t -> p kc t", p=128)[:, :, t * TT:(t + 1) * TT]
    P.op("sp", lambda e: e.dma_start(out=c.actT[:, 0:NK, :], in_=src), reads=[P.B("ao%d" % l)], writes=[P.B("actT")])
    proj_residual(c, "wout%d" % l, c.actT, P.B("actT"), NK)


def proj_residual(c, name, rhs3, rB, NK):
    cfg, P = c.cfg, c.P
    TT = cfg.TT
    for blk in range(cfg.D // WC):
        w, wB = load_wblk(c, name, blk)
        for sub in range(2):
            dc = blk * 2 + sub
            ps, psB = fm_group(c, w, wB, sub, rhs3, rB, NK)
            P.op("dve", (lambda e, ps=ps, dc=dc: e.tensor_tensor(out=c.h[:, dc, :], in0=ps[:, 0:TT], in1=c.h[:, dc, :],
                                                                  op=ALU.add)), reads=[psB, P.B("h")], writes=[P.B("h")])


def attention_phase(c, l):
    cfg, P = c.cfg, c.P
    TT, TPC, L = cfg.TT, cfg.TPC, cfg.L
    ev = c.ev[l]
    c.pb, c.pf = c.pb0, c.pf0
    NKT = (L + 127) // 128
    KT_ = [ab(c, L) for _ in range(2)]
    V_ = [ab(c, NKT * 128).rearrange("p (k d) -> p k d", d=128) for _ in range(2)]
    Q_ = [ab(c, TPC) for _ in range(2)]
    PT = [ab(c, TT) for _ in range(4)]
    OB = [ab(c, TT) for _ in range(2)]
    RI = [af(c, TT) for _ in range(2)]
    G = cfg.NH // cfg.NKV
    scale = 128.0 ** -0.5
    nfull = L // 128
    rem = L - nfull * 128
    it = 0
    for g in range(cfg.NKV):
        kt, ktB = KT_[g % 2], P.B("KT%d" % (g % 2))
        v, vB = V_[g % 2], P.B("V%d" % (g % 2))
        ksrc = ev["kf"].ap().rearrange("(g r f) t -> g f r t", r=4, f=128)[g]
        P.op("sp", (lambda e, kt=kt, ksrc=ksrc: e.dma_start(out=kt.rearrange("p (r t) -> p r t", r=4), in_=ksrc)),
             reads=[P.B("kf%d" % l)], writes=[ktB])
        for k0 in range(0, nfull, 16):
            k1 = min(nfull, k0 + 16)
            vsrc = ev["vf"].ap()[g * L + k0 * 128:g * L + k1 * 128, :].rearrange("(k i) d -> i k d", i=128)
            P.op("sp", (lambda e, v=v, vsrc=vsrc, k0=k0, k1=k1: e.dma_start(out=v[:, k0:k1, :], in_=vsrc)),
                 reads=[P.B("vf%d" % l)], writes=[vB])
        if rem:
            vsrc2 = ev["vf"].ap()[g * L + nfull * 128:(g + 1) * L, :]
            P.op("sp", (lambda e, v=v, vsrc2=vsrc2: e.dma_start(out=v[0:rem, nfull, :], in_=vsrc2)),
                 reads=[P.B("vf%d" % l)], writes=[vB])
        for hg in range(G):
            h = g * G + hg
            q, qB = Q_[h % 2], P.B("Q%d" % (h % 2))
            P.op("sp", (lambda e, q=q, h=h: e.dma_start(out=q[:, :], in_=ev["qT"].ap()[h * 128:(h + 1) * 128, :])),
                 reads=[P.B("qT%d" % l)], writes=[qB])
            for t in range(cfg.NT):
                po, poB = c.ps[4 + it % 2], P.B("ps%d" % (4 + it % 2))
                pz, pzB = c.ps[6 + it % 2], P.B("ps%d" % (6 + it % 2))
                qs = q[:, t * TT:(t + 1) * TT]

                def smm(kt_i):
                    nk = 128 if kt_i < nfull else rem
                    ps, psB = c.ps[kt_i % 4], P.B("ps%d" % (kt_i % 4))
                    P.op("pe", (lambda e, ps=ps, nk=nk, kt_i=kt_i, kt=kt, qs=qs: e.matmul(
                        ps[0:nk, 0:TT], lhsT=kt[:, kt_i * 128:kt_i * 128 + nk], rhs=qs, start=True, stop=True)),
                        reads=[ktB, qB], writes=[psB])
                    pt, ptB = PT[kt_i % 4], P.B("PT%d" % (kt_i % 4))
                    P.op("act", (lambda e, ps=ps, pt=pt, nk=nk: e.activation(
                        out=pt[0:nk, :], in_=ps[0:nk, 0:TT], func=AF.Exp, scale=scale)), reads=[psB], writes=[ptB])

                def pvmm(kt_i):
                    nk = 128 if kt_i < nfull else rem
                    pt, ptB = PT[kt_i % 4], P.B("PT%d" % (kt_i % 4))
                    P.op("pe", (lambda e, pt=pt, nk=nk, kt_i=kt_i, po=po, v=v: e.matmul(
                        po[:, 0:TT], lhsT=v[0:nk, kt_i, :], rhs=pt[0:nk, :], start=(kt_i == 0), stop=(kt_i == NKT - 1))),
                        reads=[vB, ptB], writes=[poB])
                    P.op("pe", (lambda e, pt=pt, nk=nk, kt_i=kt_i, pz=pz: e.matmul(
                        pz[:, 0:TT], lhsT=c.ones[0:nk, :], rhs=pt[0:nk, :], start=(kt_i == 0), stop=(kt_i == NKT - 1))),
                        reads=[P.B("ones"), ptB], writes=[pzB])
                SK = 2
                for kt_i in range(NKT + SK):
                    if kt_i < NKT:
                        smm(kt_i)
                    if kt_i >= SK:
                        pvmm(kt_i - SK)
                ri, riB = RI[it % 2], P.B("RI%d" % (it % 2))
                ob, obB = OB[it % 2], P.B("OB%d" % (it % 2))
                P.op("dve", (lambda e, ri=ri, pz=pz: e.reciprocal(out=ri[:, :], in_=pz[:, 0:TT])), reads=[pzB], writes=[riB])
                P.op("dve", (lambda e, ri=ri, po=po, ob=ob: e.tensor_tensor(out=ob[:, :], in0=po[:, 0:TT], in1=ri[:, :],
                                                                           op=ALU.mult)), reads=[poB, riB], writes=[obB])
                dst = ev["ao"].ap()[h * 128:(h + 1) * 128, t * TT:(t + 1) * TT]
                P.op("pool", (lambda e, ob=ob, dst=dst: e.dma_start(out=dst, in_=ob[:, :])), reads=[obB],
                     writes=[P.B("ao%d" % l)])
                it += 1


def fourier_phase(c, l):
    cfg, P = c.cfg, c.P
    L1, L2, TT, TPC, L = cfg.L1, cfg.L2, cfg.TT, cfg.TPC, cfg.L
    DL = L2 // 4
    ev = c.ev[l]
    c.pb, c.pf = c.pb0, c.pf0
    X = [ab(c, L2 * 128).rearrange("p (b d) -> p b d", d=128) for _ in range(2)]
    TW = [ab(c, L1 * 128).rearrange("p (c d) -> p c d", d=128) for _ in range(2)]
    ST = [[ab(c, 128) for _ in range(2)] for _ in range(2)]
    U = [ab(c, TPC) for _ in range(2)]
    YB = [ab(c, TT) for _ in range(2)]
    TMP = [af(c, 128) for _ in range(2)]
    a1 = ab(c, 2 * L1)
    b2 = ab(c, 3 * DL)
    cc = ab(c, 256)
    wtab = af(c, 2 * L2)
    P.op("pool", lambda e: e.dma_start(out=a1[0:L1, :], in_=c.dft_a), writes=[P.B("dfta")])
    P.op("pool", lambda e: e.dma_start(out=b2[0:L2, :], in_=c.dft_b), writes=[P.B("dftb")])
    P.op("pool", lambda e: e.dma_start(out=cc[:, :], in_=c.dft_c), writes=[P.B("dftc")])
    P.op("sp", lambda e: e.dma_start(out=wtab[0:L1, :], in_=c.dft_w), writes=[P.B("dftw")])
    tabs = [P.B("dfta"), P.B("dftb"), P.B("dftc"), P.B("dftw")]
    CPB = 512 // DL
    for cg in range(cfg.FG):
        x, xB = X[cg % 2], P.B("X%d" % (cg % 2))
        twr_d, twi_d = ev["twr"][cg % 2], ev["twi"][cg % 2]
        trB, tiB = P.B("twr%d_%d" % (l, cg % 2)), P.B("twi%d_%d" % (l, cg % 2))
        xsrc = ev["ff"].ap()[cg * L:(cg + 1) * L, :].rearrange("(a b) d -> a b d", b=L2)
        for b0 in range(0, L2, 19):
            b1 = min(L2, b0 + 19)
            P.op("sp", (lambda e, x=x, xsrc=xsrc, b0=b0, b1=b1: e.dma_start(out=x[0:L1, b0:b1, :], in_=xsrc[:, b0:b1, :])),
                 reads=[P.B("ff%d" % l)], writes=[xB])
        for b in range(L2):
            pr, prB = next_ps(c)
            pi, piB = next_ps(c)
            P.op("pe", (lambda e, pr=pr, x=x, b=b: e.matmul(pr[0:L1, 0:128], lhsT=a1[0:L1, 0:L1], rhs=x[0:L1, b, :],
                                                            start=True, stop=True)), reads=[xB] + tabs, writes=[prB])
            P.op("pe", (lambda e, pi=pi, x=x, b=b: e.matmul(pi[0:L1, 0:128], lhsT=a1[0:L1, L1:2 * L1], rhs=x[0:L1, b, :],
                                                            start=True, stop=True)), reads=[xB] + tabs, writes=[piB])
            wr, wi = wtab[0:L1, b:b + 1], wtab[0:L1, L2 + b:L2 + b + 1]
            sr_, si_ = ST[b % 2]
            srB, siB = P.B("STr%d" % (b % 2)), P.B("STi%d" % (b % 2))
            t0, t1 = TMP
            P.op("dve", (lambda e, pi=pi, wi=wi: e.tensor_scalar(out=t0[0:L1, :], in0=pi[0:L1, 0:128], scalar1=wi, scalar2=None,
                                                                 op0=ALU.mult)), reads=[piB] + tabs, writes=[P.B("tmp0")])
            P.op("dve", (lambda e, pr=pr, wr=wr, sr_=sr_: e.scalar_tensor_tensor(
                out=sr_[0:L1, :], in0=pr[0:L1, 0:128], scalar=wr, in1=t0[0:L1, :], op0=ALU.mult, op1=ALU.subtract)),
                reads=[prB, P.B("tmp0")] + tabs, writes=[srB])
            P.op("dve", (lambda e, pi=pi, wr=wr: e.tensor_scalar(out=t1[0:L1, :], in0=pi[0:L1, 0:128], scalar1=wr, scalar2=None,
                                                                 op0=ALU.mult)), reads=[piB] + tabs, writes=[P.B("tmp1")])
            P.op("dve", (lambda e, pr=pr, wi=wi, si_=si_: e.scalar_tensor_tensor(
                out=si_[0:L1, :], in0=pr[0:L1, 0:128], scalar=wi, in1=t1[0:L1, :], op0=ALU.mult, op1=ALU.add)),
                reads=[prB, P.B("tmp1")] + tabs, writes=[siB])
            P.op("pool", (lambda e, sr_=sr_, b=b, twr_d=twr_d: e.dma_start(
                out=twr_d.ap()[b, :].rearrange("(c d) -> c d", d=128), in_=sr_[0:L1, :])), reads=[srB], writes=[trB])
            P.op("pool", (lambda e, si_=si_, b=b, twi_d=twi_d: e.dma_start(
                out=twi_d.ap()[b, :].rearrange("(c d) -> c d", d=128), in_=si_[0:L1, :])), reads=[siB], writes=[tiB])
        P.op("sp", (lambda e, twr_d=twr_d: e.dma_start(out=TW[0][0:L2, :, :].rearrange("p c d -> p (c d)"), in_=twr_d.ap())),
             reads=[trB], writes=[P.B("TWr")])
        P.op("sp", (lambda e, twi_d=twi_d: e.dma_start(out=TW[1][0:L2, :, :].rearrange("p c d -> p (c d)"), in_=twi_d.ap())),
             reads=[tiB], writes=[P.B("TWi")])
        Uv = [u.rearrange("p (d c) -> p d c", c=L1) for u in U]
        c0 = 0
        while c0 < L1:
            ncg = min(CPB, L1 - c0)
            pr, prB = next_ps(c)
            pi, piB = next_ps(c)
            for ci in range(ncg):
                cc_ = c0 + ci
                sl = slice(ci * DL, (ci + 1) * DL)
                P.op("pe", (lambda e, pr=pr, cc_=cc_, sl=sl: e.matmul(pr[:, sl], lhsT=TW[0][0:L2, cc_, :], rhs=b2[0:L2, 0:DL],
                                                                        start=True, stop=False)),
                     reads=[P.B("TWr")] + tabs, writes=[prB])
                P.op("pe", (lambda e, pr=pr, cc_=cc_, sl=sl: e.matmul(pr[:, sl], lhsT=TW[1][0:L2, cc_, :], rhs=b2[0:L2, 2 * DL:3 * DL],
                                                                        start=False, stop=True)),
                     reads=[P.B("TWi")] + tabs, writes=[prB])
                P.op("pe", (lambda e, pi=pi, cc_=cc_, sl=sl: e.matmul(pi[:, sl], lhsT=TW[1][0:L2, cc_, :], rhs=b2[0:L2, 0:DL],
                                                                        start=True, stop=False)),
                     reads=[P.B("TWi")] + tabs, writes=[piB])
                P.op("pe", (lambda e, pi=pi, cc_=cc_, sl=sl: e.matmul(pi[:, sl], lhsT=TW[0][0:L2, cc_, :], rhs=b2[0:L2, DL:2 * DL],
                                                                        start=False, stop=True)),
                     reads=[P.B("TWr")] + tabs, writes=[piB])
            for (pp, ppB, k) in ((pr, prB, 0), (pi, piB, 1)):
                P.op("act", (lambda e, pp=pp, k=k, c0=c0, ncg=ncg: e.activation(
                    out=Uv[k][:, :, c0:c0 + ncg].rearrange("p d c -> p c d"),
                    in_=pp[:, 0:ncg * DL].rearrange("p (c d) -> p c d", d=DL), func=AF.Copy)),
                    reads=[ppB], writes=[P.B("U%d" % k)])
            c0 += ncg
        for t in range(cfg.NT):
            ps, psB = next_ps(c)
            P.op("pe", (lambda e, ps=ps, t=t: e.matmul(ps[:, 0:TT], lhsT=cc[:, 0:128], rhs=U[0][:, t * TT:(t + 1) * TT],
                                                       start=True, stop=False)), reads=[P.B("U0")] + tabs, writes=[psB])
            P.op("pe", (lambda e, ps=ps, t=t: e.matmul(ps[:, 0:TT], lhsT=cc[:, 128:256], rhs=U[1][:, t * TT:(t + 1) * TT],
                                                       start=False, stop=True)), reads=[P.B("U1")] + tabs, writes=[psB])
            yb, ybB = YB[t % 2], P.B("YB%d" % (t % 2))
            P.op("act", (lambda e, ps=ps, yb=yb: e.activation(out=yb[:, :], in_=ps[:, 0:TT], func=AF.Copy)),
                 reads=[psB], writes=[ybB])
            dst = ev["ao"].ap()[cfg.QW + cg * 128:cfg.QW + (cg + 1) * 128, t * TT:(t + 1) * TT]
            P.op("pool", (lambda e, yb=yb, dst=dst: e.dma_start(out=dst, in_=yb[:, :])), reads=[ybB], writes=[P.B("ao%d" % l)])


def odd_in_tile(c, l, t):
    cfg, P = c.cfg, c.P
    KC, TT = cfg.KC, cfg.TT
    o_i = l // 2
    od = c.od[l]
    hnB = P.B("hn")
    rmsnorm_tile(c, (l * 3 + 1) * KC)
    store_h(c, c.hT, t)
    name = "win%d" % l
    nq, nk, nv = cfg.GKW // WC, cfg.GKW // WC, cfg.GVW // WC
    tsl = slice(t * TT, (t + 1) * TT)
    for blk in range(cfg.OIN // WC):
        w, wB = load_wblk(c, name, blk)
        if blk < nq + nk:
            isq = blk < nq
            for sub in range(2):
                ps, psB = fm_group(c, w, wB, sub, c.hn, hnB, KC)
                row = ((blk if isq else blk - nq) * 2 + sub) * 128
                key = "qs" if isq else "ks"
                stage_out(c, ps, psB, 128, TT, od[key].ap()[row:row + 128, tsl], P.B("%s%d" % (key, l)))
        if nq <= blk < nq + nk + nv:
            isk = blk < nq + nk
            col0 = (blk - nq) * WC if isk else (blk - nq - nk) * WC
            key = "kts" if isk else "vts"
            for (tok0, ntok) in subtiles(TT):
                ps, psB = tm_group(c, w, wB, tok0, ntok)
                r0 = t * TT + tok0
                gw = (cfg.GKW if isk else cfg.GVW) // 4
                FS = 4 if isk else 8
                fw = gw // FS
                jj, off = col0 // gw, col0 % gw
                nfq = WC // fw
                dstv = od[key].ap().rearrange("(j q t) f -> j t q f", j=4, q=FS)[jj, r0:r0 + ntok, off // fw:off // fw + nfq, :]
                stage_out(c, ps, psB, ntok, WC, dstv, P.B("%s%d" % (key, l)), split=nfq)
        if blk >= nq + nk + nv:
            for sub in range(2):
                ps, psB = fm_group(c, w, wB, sub, c.hn, hnB, KC)
                row = ((blk - nq - nk - nv) * 2 + sub) * 128
                stage_out(c, ps, psB, 128, TT, od["sr"].ap()[row:row + 128, tsl], P.B("sr%d" % l), func=AF.Silu)
    for n in range(2):
        ps, psB = next_ps(c)
        for kc in range(KC):
            a0 = (n * KC + kc) * 16
            P.op("pe", (lambda e, ps=ps, kc=kc, a0=a0: e.matmul(ps[0:16, 0:TT], lhsT=c.ga[:, a0:a0 + 16], rhs=c.hn[:, kc, :],
                                                                 start=(kc == 0), stop=(kc == KC - 1))),
                 reads=[hnB, P.B("ga")], writes=[psB])
        P.op("act", (lambda e, ps=ps: e.activation(out=c.lowx[0:16, :], in_=ps[0:16, 0:TT], func=AF.Copy)),
             reads=[psB], writes=[P.B("lowx")])
        key = "gfs" if n == 0 else "gbs"
        for (tok0, ntok) in subtiles(TT):
            for cb in range(cfg.GKW // 512):
                g0 = n * cfg.GKW + cb * 512
                pz, pzB = next_ps(c)
                P.op("pe", (lambda e, pz=pz, tok0=tok0, ntok=ntok, g0=g0: e.matmul(
                    pz[0:ntok, 0:512], lhsT=c.lowx[0:17, tok0:tok0 + ntok], rhs=c.gbx[0:17, g0:g0 + 512], start=True, stop=True)),
                    reads=[P.B("lowx"), P.B("gbx")], writes=[pzB])
                P.op("act", (lambda e, pz=pz, ntok=ntok: e.activation(out=c.ef[0:ntok, :], in_=pz[0:ntok, 0:512], func=AF.Exp,
                                                                      scale=-1.0)), reads=[pzB], writes=[P.B("ef")])
                i = c.vsti
                c.vsti ^= 1
                st_, stB = c.vst[i], P.B("vst%d" % i)
                P.op("act", (lambda e, st_=st_, ntok=ntok: e.activation(out=st_[0:ntok, 0:512], in_=c.ef[0:ntok, :], func=AF.Ln,
                                                                        bias=1.0)), reads=[P.B("ef")], writes=[stB])
                r0 = t * TT + tok0
                gw = cfg.GKW // 4
                pw = min(gw, 512)
                for pc in range(512 // pw):
                    col = cb * 512 + pc * pw
                    jj, off = col // gw, col % gw
                    fw = gw // 4
                    nfq = pw // fw
                    dst = od[key].ap().rearrange("(j q t) f -> j t q f", j=4, q=4)[jj, r0:r0 + ntok, off // fw:off // fw + nfq, :]
                    P.op("pool", (lambda e, st_=st_, ntok=ntok, dst=dst, pc=pc, pw=pw, nfq=nfq: e.dma_start(
                        out=dst, in_=st_[0:ntok, pc * pw:(pc + 1) * pw].rearrange("p (q f) -> p q f", q=nfq))),
                        reads=[stB], writes=[P.B("%s%d" % (key, l))])


def odd_out_tile(c, l, t):
    cfg, P = c.cfg, c.P
    TT, TPC = cfg.TT, cfg.TPC
    o_i = l // 2
    od = c.od[l]
    NK = cfg.GVW // 128
    NDV = c.NDV
    nfo = c.HPC * NDV
    for r in range(4):
        ofv = od["ol"].ap().rearrange("(fc r p) t -> r p fc t", r=4, p=128)[r][:, :, t * TT:(t + 1) * TT]
        P.op("sp", (lambda e, r=r, ofv=ofv: e.dma_start(out=c.actT[:, r * nfo:(r + 1) * nfo, :], in_=ofv)),
             reads=[P.B("ol%d" % l)], writes=[P.B("actT")])
    srv = od["sr"].ap().rearrange("(kc p) t -> p kc t", p=128)[:, :, t * TT:(t + 1) * TT]
    P.op("sp", lambda e: e.dma_start(out=c.srt[:, 0:NK, :], in_=srv), reads=[P.B("sr%d" % l)], writes=[P.B("wbuf1")])
    P.op("act", lambda e: e.activation(out=c.hn[:, 0:NK, :], in_=c.actT[:, 0:NK, :], func=AF.Square),
         reads=[P.B("actT")], writes=[P.B("hn")])
    for hh in range(cfg.GH):
        ps, psB = next_ps(c)
        for ec in range(NDV):
            P.op("pe", (lambda e, ps=ps, ec=ec, hh=hh: e.matmul(ps[:, 0:TT], lhsT=c.ones[:, :], rhs=c.hn[:, hh * NDV + ec, :],
                                                                start=(ec == 0), stop=(ec == NDV - 1))),
                 reads=[P.B("hn"), P.B("ones")], writes=[psB])
        rstd_from(c, ps, psB, c.rstdh, P.B("rstdh"), 1.0 / cfg.GDV, TT)
        for ec in range(NDV):
            k = hh * NDV + ec
            P.op("dve", (lambda e, k=k, ec=ec: e.scalar_tensor_tensor(
                out=c.hn[:, k, :], in0=c.actT[:, k, :], scalar=c.hg[:, o_i * NDV + ec:o_i * NDV + ec + 1], in1=c.rstdh[:, :],
                op0=ALU.mult, op1=ALU.mult)), reads=[P.B("actT"), P.B("rstdh"), P.B("hg")], writes=[P.B("hn")])
            P.op("dve", (lambda e, k=k: e.tensor_tensor(out=c.hn[:, k, :], in0=c.hn[:, k, :], in1=c.srt[:, k, :], op=ALU.mult)),
                 reads=[P.B("hn"), P.B("wbuf1")], writes=[P.B("hn")])
    proj_residual(c, "wout%d" % l, c.hn, P.B("hn"), NK)


def gla_phase(c, l):
    cfg, P = c.cfg, c.P
    TPC, L = cfg.TPC, cfg.L
    GDK, GDV, GKW, GVW = cfg.GDK, cfg.GDV, cfg.GKW, cfg.GVW
    NDK, NDV, HPC, GCB = c.NDK, c.NDV, c.HPC, c.GCB
    od = c.od[l]
    c.pb, c.pf = c.pb0, c.pf0
    QT = ab(c, NDK * L).rearrange("p (k t) -> p k t", t=L)
    KT = ab(c, NDK * L).rearrange("p (k t) -> p k t", t=L)
    KTOK = ab(c, GCB * GDK).rearrange("p (c d) -> p c d", d=GDK)
    VTOK = ab(c, GCB * GDV).rearrange("p (c d) -> p c d", d=GDV)
    GTOK = ab(c, GCB * GDK).rearrange("p (c d) -> p c d", d=GDK)
    OB = ab(c, NDV * GCB * 64).rearrange("p (e t) -> p e t", e=NDV)
    OF = ab(c, NDV * GCB * 64).rearrange("p (e t) -> p e t", e=NDV)
    QTL = ab(c, NDK * 64).rearrange("p (k t) -> p k t", t=64)
    KTL = ab(c, NDK * 64).rearrange("p (k t) -> p k t", t=64)
    KH = ab(c, GDK)
    ATM = ab(c, 64)
    SBF = ab(c, NDK * GDV).rearrange("p (k e) -> p k e", e=GDV)
    MK = ab(c, 6 * 64)
    S = af(c, NDK * GDV).rearrange("p (k e) -> p k e", e=GDV)
    EQ = af(c, NDK * 64).rearrange("p (k t) -> p k t", t=64)
    EK = af(c, NDK * 64).rearrange("p (k t) -> p k t", t=64)
    ER = af(c, GDK)
    EBL = af(c, NDK)
    P.op("pool", lambda e: e.dma_start(out=MK[0:64, :], in_=c.gla_m), writes=[P.B("MK")])
    B = P.B
    lnscale = float(np.log(GDK ** -0.5))
    NBLK = (cfg.NREAL // 64) // GCB
    pB = [B("ps%d" % i) for i in range(8)]
    def do_chunk(hl, d_i, ci, n, pos, M1, M2, M01):
        g_ = GTOK[0:n, ci, :]
        P.op("pe", (lambda e: e.matmul(c.ps[0][0:n, 0:GDK], lhsT=M2[0:n, 0:n], rhs=g_, start=True, stop=True)),
             reads=[B("GTOK"), B("MK")], writes=[pB[0]])
        for dk in range(NDK):
            P.op("pe", (lambda e, dk=dk: e.matmul(c.ps[2][:, dk * 64:dk * 64 + n], lhsT=GTOK[0:n, ci, dk * 128:(dk + 1) * 128],
                                                  rhs=M1[0:n, 0:n], start=True, stop=True)),
                 reads=[B("GTOK"), B("MK")], writes=[pB[2]])
        bT = c.ps[2][:, 0:NDK * 64].rearrange("p (k t) -> p k t", t=64)[:, :, 0:n]
        P.op("act", lambda e: e.activation(out=EQ[:, :, 0:n], in_=bT, func=AF.Exp, bias=lnscale), reads=[pB[2]], writes=[B("EQ")])
        P.op("act", lambda e: e.activation(out=EK[:, :, 0:n], in_=bT, func=AF.Exp, scale=-1.0), reads=[pB[2]], writes=[B("EK")])
        il = (n - 1) if d_i == 0 else 0
        P.op("act", (lambda e: e.activation(
            out=EBL[:, :], in_=c.ps[2][:, 0:NDK * 64].rearrange("p (k t) -> p k t", t=64)[:, :, il], func=AF.Exp)),
            reads=[pB[2]], writes=[B("EBL")])
        P.op("act", lambda e: e.activation(out=ER[0:n, :], in_=c.ps[0][0:n, 0:GDK], func=AF.Exp), reads=[pB[0]], writes=[B("ER")])
        P.op("dve", (lambda e: e.tensor_tensor(out=QTL[:, :, 0:n], in0=QT[:, :, pos:pos + n], in1=EQ[:, :, 0:n], op=ALU.mult)),
             reads=[B("QT"), B("EQ")], writes=[B("QTL")])
        P.op("dve", (lambda e: e.tensor_tensor(out=KTL[:, :, 0:n], in0=KT[:, :, pos:pos + n], in1=EK[:, :, 0:n], op=ALU.mult)),
             reads=[B("KT"), B("EK")], writes=[B("KTL")])
        P.op("dve", (lambda e: e.tensor_tensor(out=KH[0:n, :], in0=KTOK[0:n, ci, :], in1=ER[0:n, :], op=ALU.mult)),
             reads=[B("KTOK"), B("ER")], writes=[B("KH")])
        for dk in range(NDK):
            P.op("pe", (lambda e, dk=dk: e.matmul(c.ps[3][0:n, 0:n], lhsT=KTL[:, dk, 0:n], rhs=QTL[:, dk, 0:n],
                                                  start=(dk == 0), stop=(dk == NDK - 1))),
                 reads=[B("KTL"), B("QTL")], writes=[pB[3]])
        P.op("dve", lambda e: e.tensor_tensor(out=ATM[0:n, 0:n], in0=c.ps[3][0:n, 0:n], in1=M01[0:n, 0:n], op=ALU.mult),
             reads=[pB[3], B("MK")], writes=[B("ATM")])
        for ec in range(NDV):
            P.op("pe", (lambda e, ec=ec: e.matmul(c.ps[4][:, ec * 64:ec * 64 + n], lhsT=VTOK[0:n, ci, ec * 128:(ec + 1) * 128],
                                                  rhs=ATM[0:n, 0:n], start=True, stop=False)),
                 reads=[B("VTOK"), B("ATM")], writes=[pB[4]])
            for dk in range(NDK):
                P.op("pe", (lambda e, ec=ec, dk=dk: e.matmul(c.ps[4][:, ec * 64:ec * 64 + n], lhsT=SBF[:, dk, ec * 128:(ec + 1) * 128],
                                                             rhs=QTL[:, dk, 0:n], start=False, stop=(dk == NDK - 1))),
                     reads=[B("SBF"), B("QTL")], writes=[pB[4]])
        ov = c.ps[4][:, 0:NDV * 64].rearrange("p (e t) -> p e t", t=64)[:, :, 0:n]
        if d_i == 0:
            P.op("act", (lambda e: e.activation(out=OB[:, :, ci * n:(ci + 1) * n], in_=ov, func=AF.Copy)),
                 reads=[pB[4]], writes=[B("OB")])
        else:
            P.op("dve", (lambda e: e.tensor_tensor(out=OB[:, :, ci * n:(ci + 1) * n], in0=ov, in1=OF[:, :, ci * n:(ci + 1) * n],
                                                   op=ALU.add)), reads=[pB[4], B("OF")], writes=[B("OB")])
        for dk in range(NDK):
            P.op("pe", (lambda e, dk=dk: e.matmul(c.ps[5 + dk][:, 0:GDV], lhsT=KH[0:n, dk * 128:(dk + 1) * 128],
                                                  rhs=VTOK[0:n, ci, :], start=True, stop=True)),
                 reads=[B("KH"), B("VTOK")], writes=[pB[5 + dk]])
            P.op("dve", (lambda e, dk=dk: e.scalar_tensor_tensor(
                out=S[:, dk, :], in0=S[:, dk, :], scalar=EBL[:, dk:dk + 1], in1=c.ps[5 + dk][:, 0:GDV],
                op0=ALU.mult, op1=ALU.add)), reads=[B("S"), B("EBL"), pB[5 + dk]], writes=[B("S")])
        P.op("act", lambda e: e.activation(out=SBF[:, :, :], in_=S[:, :, :], func=AF.Copy), reads=[B("S")], writes=[B("SBF")])

    def do_block(hl, d_i, bi, M1, M2, M01, gkey):
        if bi < 0:
            p0, n, nch = 0, 16, 1
        else:
            p0, n, nch = 16 + bi * GCB * 64, 64, GCB
        npos = n * nch

        def ld(dstT, key, wdt, bn):
            FS = 8 if key == "vtl" else 4
            fw = HPC * wdt // FS
            for qq in range(wdt // fw):
                qi = hl * (wdt // fw) + qq
                P.op("sp", (lambda e, qq=qq, qi=qi: e.dma_start(
                    out=dstT[0:n, 0:nch, qq * fw:(qq + 1) * fw],
                    in_=od[key].ap()[qi * L + p0:qi * L + p0 + npos, :].rearrange("(c i) d -> i c d", i=n))),
                    reads=[B("%s%d" % (key, l))], writes=[B(bn)])
        ld(KTOK, "ktl", GDK, "KTOK")
        ld(VTOK, "vtl", GDV, "VTOK")
        ld(GTOK, gkey, GDK, "GTOK")
        if d_i == 1:
            P.op("sp", (lambda e: e.dma_start(
                out=OF[:, :, 0:npos],
                in_=od["ofw"].ap()[hl * GDV:(hl + 1) * GDV, p0:p0 + npos].rearrange("(e p) t -> p e t", p=128))),
                reads=[B("ofw%d" % l)], writes=[B("OF")])
        chunks = list(range(nch))
        if d_i == 1:
            chunks = chunks[::-1]
        for ci in chunks:
            do_chunk(hl, d_i, ci, n, p0 + ci * n, M1, M2, M01)
        if d_i == 0:
            dst = od["ofw"].ap()[hl * GDV:(hl + 1) * GDV, p0:p0 + npos].rearrange("(e p) t -> p e t", p=128)
            P.op("pool", (lambda e: e.dma_start(out=dst, in_=OB[:, :, 0:npos])), reads=[B("OB")], writes=[B("ofw%d" % l)])
        else:
            osv = od["os"].ap().rearrange("(fc q p) t -> fc q p t", q=4, p=128)
            a0 = p0
            while a0 < p0 + npos:
                q_ = a0 // TPC
                a1 = min(p0 + npos, (q_ + 1) * TPC)
                dst = osv[hl * NDV:(hl + 1) * NDV, q_, :, a0 - q_ * TPC:a1 - q_ * TPC].rearrange("e p t -> p e t")
                P.op("pool", (lambda e, dst=dst, a0=a0, a1=a1: e.dma_start(out=dst, in_=OB[:, :, a0 - p0:a1 - p0])),
                     reads=[B("OB")], writes=[B("os%d" % l)])
                a0 = a1

    def do_head(hl):
        def ldT(dstT, key, bn, dk, r):
            row = ((hl * NDK + dk) * 4 + r) * 128
            P.op("sp", (lambda e: e.dma_start(
                out=dstT[:, dk, r * TPC:(r + 1) * TPC], in_=od[key].ap()[row:row + 128, :])),
                reads=[B("%s%d" % (key, l))], writes=[B(bn)])
        for dk in range(NDK):
            for r in range(4):
                ldT(QT, "ql", "QT", dk, r)
                ldT(KT, "kl", "KT", dk, r)
        for d_i in range(2):
            m0 = d_i * 192
            M1, M2, M01 = MK[0:64, m0:m0 + 64], MK[0:64, m0 + 64:m0 + 128], MK[0:64, m0 + 128:m0 + 192]
            gkey = "gfl" if d_i == 0 else "gbl"
            P.op("dve", lambda e: e.memset(S[:, :, :], 0.0), writes=[B("S")])
            P.op("act", lambda e: e.activation(out=SBF[:, :, :], in_=S[:, :, :], func=AF.Copy), reads=[B("S")], writes=[B("SBF")])
            blocks = list(range(-1, NBLK))
            if d_i == 1:
                blocks = blocks[::-1]
            for bi in blocks:
                do_block(hl, d_i, bi, M1, M2, M01, gkey)

    for hl in range(HPC):
        do_head(hl)


def program(c):
    cfg, P, nc = c.cfg, c.P, c.nc
    KC = cfg.KC
    ne, no = (cfg.DEPTH + 1) // 2, cfg.DEPTH // 2
    c.ones = ab(c, 128)
    c.rotm = ab(c, 128)
    c.ga = ab(c, 2 * KC * 16)
    c.gbx = ab(c, 2 * cfg.GKW)
    c.pn_sb = af(c, cfg.DEPTH * 3 * KC)
    c.qkg = af(c, 2 * max(ne, 1))
    c.hg = af(c, max(no, 1) * c.NDV)
    c.pb0, c.pf0 = c.pb, c.pf
    tl_setup(c)
    P.op("dve", lambda e: e.memset(c.ones[:, :], 1.0), writes=[P.B("ones")])
    P.op("dve", lambda e: e.memset(c.lowx[0:32, :], 1.0), writes=[P.B("lowx")])
    P.op("sp", lambda e: e.dma_start(out=c.pn_sb[:, :], in_=c.pn), writes=[P.B("pn_sb")])
    P.op("sp", lambda e: e.dma_start(out=c.qkg[:, :], in_=c.qkg_d), writes=[P.B("qkg")])
    P.op("sp", lambda e: e.dma_start(out=c.hg[:, :], in_=c.hg_d), writes=[P.B("hg")])
    P.op("pool", lambda e: e.dma_start(out=c.rotm[:, :], in_=c.rotm_d), writes=[P.B("rotm")])
    phase0_weights(c)
    full = cfg.stages != "ffn"
    for l in range(cfg.DEPTH + 1):
        if l > 0:
            P.fence()
            tl_setup(c)
            if full:
                P.op("dve", lambda e: e.memset(c.lowx[0:32, :], 1.0), writes=[P.B("lowx")])
        if l < cfg.DEPTH and l % 2 == 1 and full:
            o_i = l // 2
            na, nb = 2 * KC * 16, 2 * cfg.GKW
            P.op("pool", (lambda e, o_i=o_i, na=na: e.dma_start(out=c.ga[:, :], in_=c.ga_d[:, o_i * na:(o_i + 1) * na])),
                 writes=[P.B("ga")])
            P.op("pool", (lambda e, o_i=o_i, nb=nb: e.dma_start(out=c.gbx[0:17, :], in_=c.gb_d[:, o_i * nb:(o_i + 1) * nb])),
                 writes=[P.B("gbx")])
        for t in range(cfg.NT):
            if l == 0:
                load_h(c, c.xT, t)
            else:
                load_h(c, c.hT, t)
                if full:
                    if "noout" in cfg.stages:
                        pass
                    elif (l - 1) % 2 == 0:
                        even_out_tile(c, l - 1, t)
                    else:
                        odd_out_tile(c, l - 1, t)
                ffn_tile(c, l - 1, 1)
            if l < cfg.DEPTH:
                ffn_tile(c, l, 0)
                if full:
                    if "noin" in cfg.stages:
                        store_h(c, c.hT, t)
                    elif l % 2 == 0:
                        even_in_tile(c, l, t)
                    else:
                        odd_in_tile(c, l, t)
                else:
                    store_h(c, c.hT, t)
            else:
                store_h(c, c.outT, t)
        if l < cfg.DEPTH and full:
            if l % 2 == 0:
                ev = c.ev[l]
                cag(c, ev["ks"], ev["kf"], cfg.NKV, "ks%d" % l, "kf%d" % l)
                cag(c, ev["vs"], ev["vf"], cfg.NKV, "vs%d" % l, "vf%d" % l)
                cag(c, ev["fs"], ev["ff"], cfg.FG, "fs%d" % l, "ff%d" % l)
                P.fence()
                if "noattn" not in cfg.stages:
                    attention_phase(c, l)
                P.fence()
                if "nofft" not in cfg.stages:
                    fourier_phase(c, l)
            else:
                od = c.od[l]
                nfc = cfg.GKW // 128
                for nm in ("q", "k"):
                    cag(c, od[nm + "s"], od[nm + "f"], nfc, "%ss%d" % (nm, l), "%sf%d" % (nm, l))
                for nm in ("kt", "vt", "gf", "gb"):
                    cag(c, od[nm + "s"], od[nm + "f"], 4 * (8 if nm == "vt" else 4), "%ss%d" % (nm, l), "%sf%d" % (nm, l))
                for (src, dst) in (("qf", "ql"), ("kf", "kl")):
                    P.op("sp", (lambda e, src=src, dst=dst, od=od, l=l: e.dma_start(
                        out=od[dst].ap(),
                        in_=od[src].ap().rearrange("(j x) t -> j x t", j=4)[bass.ds(jv(c, l), 1), :, :].squeeze(0))),
                        reads=[P.B("%s%d" % (src, l))], writes=[P.B("%s%d" % (dst, l))])
                for (src, dst) in (("ktf", "ktl"), ("vtf", "vtl"), ("gff", "gfl"), ("gbf", "gbl")):
                    P.op("sp", (lambda e, src=src, dst=dst, od=od, l=l: e.dma_start(
                        out=od[dst].ap(),
                        in_=od[src].ap().rearrange("(j x) f -> j x f", j=4)[bass.ds(jv(c, l), 1), :, :].squeeze(0))),
                        reads=[P.B("%s%d" % (src, l))], writes=[P.B("%s%d" % (dst, l))])
                P.fence()
                gla_phase(c, l)
                nfo = c.HPC * c.NDV
                cag(c, od["os"], od["of"], nfo * 4, "os%d" % l, "of%d" % l)
                P.op("sp", (lambda e, od=od, nfo=nfo, l=l: e.dma_start(
                    out=od["ol"].ap().rearrange("(fc x) t -> fc x t", fc=nfo),
                    in_=od["of"].ap().rearrange("(fc q r p) t -> fc q r p t", q=4, r=4, p=128)[:, bass.ds(jv(c, l), 1), :, :, :].squeeze(1).rearrange("fc r p t -> fc (r p) t"))),
                    reads=[P.B("of%d" % l)], writes=[P.B("ol%d" % l)])


def const_tables(cfg):
    L, L1, L2, TPC = cfg.L, cfg.L1, cfg.L2, cfg.TPC
    f64 = np.float64
    t = {}
    nfreq = 32
    inv = np.power(10000.0, -np.arange(nfreq, dtype=np.float32) / nfreq).astype(np.float32)
    pos = np.arange(cfg.NREAL)
    rows = (pos // cfg.GRID_W).astype(np.float32)
    cols = (pos % cfg.GRID_W).astype(np.float32)
    ang = np.zeros((L, 2, nfreq), np.float32)
    ang[cfg.NMETA:, 0] = rows[:, None] * inv
    ang[cfg.NMETA:, 1] = cols[:, None] * inv
    cos = np.cos(ang)
    sin = np.sin(ang)
    cosT = np.zeros((128, L), np.float32)
    sinT = np.zeros((128, L), np.float32)
    for d in range(128):
        ax, half, fr = d // 64, (d % 64) // 32, d % 32
        cosT[d] = cos[:, ax, fr]
        sinT[d] = sin[:, ax, fr] * (-1.0 if half == 0 else 1.0)
    t["cosT"], t["sinT"] = cosT, sinT
    rot = np.zeros((128, 128), np.float32)
    for m in range(128):
        partner = m + 32 if (m % 64) < 32 else m - 32
        rot[partner, m] = 1.0
    t["rotm"] = rot
    a = np.arange(L1, dtype=f64)
    t["dft_a"] = np.concatenate([np.cos(2 * np.pi * np.outer(a, a) / L1), np.sin(2 * np.pi * np.outer(a, a) / L1)], 1).astype(np.float32)
    b = np.arange(L2, dtype=f64)
    t["dft_w"] = np.concatenate([np.cos(2 * np.pi * np.outer(a, b) / L), np.sin(2 * np.pi * np.outer(a, b) / L)], 1).astype(np.float32)
    c2, s2 = np.cos(2 * np.pi * np.outer(b, b) / L2), np.sin(2 * np.pi * np.outer(b, b) / L2)
    DL = L2 // 4
    t["dft_b"] = [np.concatenate([c2[:, j * DL:(j + 1) * DL], s2[:, j * DL:(j + 1) * DL], -s2[:, j * DL:(j + 1) * DL]], 1).astype(np.float32)
                  for j in range(4)]
    ch = np.arange(128, dtype=f64)
    s = 1.0 / np.sqrt(L * 128.0)
    t["dft_c"] = np.concatenate([np.cos(2 * np.pi * np.outer(ch, ch) / 128) * s, -np.sin(2 * np.pi * np.outer(ch, ch) / 128) * s], 1).astype(np.float32)
    jj, ii = np.meshgrid(np.arange(64), np.arange(64), indexing="ij")
    tau = 16.0
    fw = [(jj <= ii) * (-1.0 / tau), (jj > ii) * (-1.0 / tau), (jj <= ii) * 1.0]
    bw = [(jj >= ii) * (-1.0 / tau), (jj < ii) * (-1.0 / tau), (jj >= ii) * 1.0]
    t["gla_m"] = np.concatenate(fw + bw, 1).astype(np.float32)
    return t


def host_inputs(cfg, inp):
    x = np.asarray(inp["x"], np.float32)
    meta = np.asarray(inp["meta_tokens"], np.float32)
    KC = cfg.KC
    ne, no = (cfg.DEPTH + 1) // 2, cfg.DEPTH // 2
    maps = [dict() for _ in range(NCORES)]
    tabs = const_tables(cfg)
    for b in range(2):
        hb = np.concatenate([meta, x[b]], axis=0)
        for j in range(4):
            m = maps[b * 4 + j]
            sl = slice(j * cfg.TPC, (j + 1) * cfg.TPC)
            m["xT"] = np.ascontiguousarray(hb[sl].T)
            m["rope"] = np.ascontiguousarray(np.concatenate([tabs["cosT"][:, sl], tabs["sinT"][:, sl]], 1))
            m["dft_b"] = tabs["dft_b"][j]
    pn = np.asarray(inp["pre_norm"], np.float32)
    pn_t = np.ascontiguousarray(pn.reshape(cfg.DEPTH * 3, KC, 128).transpose(2, 0, 1)).reshape(128, -1)
    qkg = np.zeros((128, 2 * max(ne, 1)), np.float32)
    for e in range(ne):
        qkg[:, 2 * e] = np.asarray(inp["even_q_norm"], np.float32)[e]
        qkg[:, 2 * e + 1] = np.asarray(inp["even_k_norm"], np.float32)[e]
    ga = np.zeros((128, max(no, 1) * 2 * KC * 16), np.float32)
    gb = np.zeros((17, max(no, 1) * 2 * cfg.GKW), np.float32)
    hg = np.zeros((128, max(no, 1) * (cfg.GDV // 128)), np.float32)
    if no:
        A = np.asarray(inp["odd_gate_a"], np.float32)
        ga = np.ascontiguousarray(A.reshape(no, 2, KC, 128, 16).transpose(3, 0, 1, 2, 4)).reshape(128, -1)
        Bm = np.asarray(inp["odd_gate_b"], np.float32)
        bias = np.asarray(inp["odd_gate_bias"], np.float32)
        gb = np.ascontiguousarray(np.concatenate([Bm, bias[:, :, None, :]], axis=2).transpose(2, 0, 1, 3)).reshape(17, -1)
        H = np.asarray(inp["odd_head_norm"], np.float32)
        hg = np.ascontiguousarray(H.reshape(no, cfg.GDV // 128, 128).transpose(2, 0, 1)).reshape(128, -1)
    W = {}
    for l in range(cfg.DEPTH):
        for i in range(2):
            W["wg%d_%d" % (l, i)] = inp["ffn_w_gate"][l, i]
            W["wu%d_%d" % (l, i)] = inp["ffn_w_up"][l, i]
            W["wd%d_%d" % (l, i)] = inp["ffn_w_down"][l, i]
        if l % 2 == 0:
            W["win%d" % l] = inp["even_w_in"][l // 2]
            W["wout%d" % l] = inp["even_w_out"][l // 2]
        else:
            W["win%d" % l] = inp["odd_w_in"][l // 2]
            W["wout%d" % l] = inp["odd_w_out"][l // 2]
    for name, K, M in weight_specs(cfg):
        wt = tile_weight(np.asarray(W[name], np.float32)).reshape(-1)
        n8 = wt.size // 8
        pe = min(WPE, n8)
        w3 = wt.reshape(n8 // pe, 8, pe)
        for cidx in range(NCORES):
            k = CHUNK_ORDER.index(cidx)
            maps[cidx][name] = np.ascontiguousarray(w3[:, k, :]).reshape(n8 // CW, CW)
        del wt, w3
    for m in maps:
        m["pn"], m["qkg"], m["ga"], m["gb"], m["hg"] = pn_t, qkg, ga, gb, hg
        m["rotm"], m["dft_a"], m["dft_w"], m["dft_c"], m["gla_m"] = tabs["rotm"], tabs["dft_a"], tabs["dft_w"], tabs["dft_c"], tabs["gla_m"]
    return maps


def run(cfg, inp):
    nc = build(cfg)
    maps = host_inputs(cfg, inp)
    res = run_bass_kernel_spmd(nc, maps, core_ids=list(range(NCORES)))
    out = np.empty((2, cfg.L, cfg.D), np.float32)
    for b in range(2):
        for j in range(4):
            out[b, j * cfg.TPC:(j + 1) * cfg.TPC] = np.asarray(res.results[b * 4 + j]["outT"]).T
    return out[:, cfg.NMETA:]


def kernel(**inputs):
    return run(Cfg(), inputs)
```
